# Optimizing a Trainium2 kernel written in Bass

```python
import jax, jax.numpy as jnp
from jax import lax
import numpy as np

D_MODEL = 2048
BATCH = 4
SEQ = 2048
DEPTH = 2

D_GROUP = D_MODEL // 2
HEAD_DIM = 128
N_MLSTM = D_GROUP // HEAD_DIM
RWKV_HEAD = 64
N_RWKV = D_GROUP // RWKV_HEAD
N_RET = D_GROUP // HEAD_DIM
N_GDN = D_GROUP // HEAD_DIM
RWKV_DECAY_RANK = max(32, int(round(1.8 * D_GROUP ** 0.5 / 32)) * 32)
RWKV_AAA_RANK = RWKV_DECAY_RANK
RWKV_GATE_RANK = max(32, int(round(0.6 * D_GROUP ** 0.8 / 32)) * 32)
CONV_WIDTH = 4
CHUNK = 64
D_FF = ((8 * D_MODEL + 767) // 768) * 256
ALPHA = (2 * DEPTH) ** 0.25
BETA = (8 * DEPTH) ** -0.25
LN_EPS = 1e-5
RWKV_LN_EPS = 64e-5
ROPE_BASE = 10000.0
RET_GAMMA_BASE = 5.0

A_SIZES = (D_GROUP, D_GROUP, D_GROUP, D_GROUP, N_MLSTM, N_MLSTM)
B_SIZES = (D_GROUP, D_GROUP, D_GROUP, RWKV_DECAY_RANK, RWKV_AAA_RANK, RWKV_GATE_RANK)
C_SIZES = (D_GROUP, D_GROUP, D_GROUP, D_GROUP)
D_SIZES = (D_GROUP, D_GROUP, D_GROUP, D_GROUP, N_GDN, N_GDN)
A_COLS = sum(A_SIZES)
B_COLS = sum(B_SIZES)
C_COLS = sum(C_SIZES)
D_COLS = sum(D_SIZES)

kernel_name = 'hybrid_mlstm_rwkv7_retnet_gdn_deepnorm_adaln'


def _split(z, sizes):
    return jnp.split(z, np.cumsum(sizes)[:-1].tolist(), axis=-1)


def _layer_norm(x, g, b):
    xf = x.astype(jnp.float32)
    mu = xf.mean(-1, keepdims=True)
    var = jnp.square(xf - mu).mean(-1, keepdims=True)
    return ((xf - mu) * lax.rsqrt(var + LN_EPS) * g + b).astype(x.dtype)


def _head_norm(h, g, b=None, eps=LN_EPS, center=True):
    hf = h.astype(jnp.float32)
    if center:
        hf = hf - hf.mean(-1, keepdims=True)
    hf = hf * lax.rsqrt(jnp.square(hf).mean(-1, keepdims=True) + eps)
    out = hf.reshape(h.shape[0], h.shape[1], -1) * g
    return out if b is None else out + b


def _l2norm(z):
    return z * lax.rsqrt(jnp.sum(jnp.square(z), -1, keepdims=True) + 1e-6)


def _token_shift(z):
    return jnp.pad(z[:, :-1], ((0, 0), (1, 0), (0, 0)))


def _causal_conv(z, w):
    return lax.conv_general_dilated(z, w[:, None, :].astype(z.dtype), window_strides=(1,),
                                    padding=[(w.shape[0] - 1, 0)],
                                    dimension_numbers=('NWC', 'WIO', 'NWC'),
                                    feature_group_count=z.shape[-1])


def _heads(z, n_heads):
    B, T, _ = z.shape
    return z.reshape(B, T, n_heads, -1).transpose(0, 2, 1, 3).astype(jnp.float32)


def _chunk(z):
    B, H, T = z.shape[:3]
    return jnp.moveaxis(z.reshape(B, H, T // CHUNK, CHUNK, *z.shape[3:]), 2, 0)


def _unchunk(z):
    nc, B, H, L, d = z.shape
    return jnp.moveaxis(z, 0, 2).reshape(B, H, nc * L, d).transpose(0, 2, 1, 3)


def _rotary(z, positions):
    half = z.shape[-1] // 2
    inv_freq = ROPE_BASE ** (-jnp.arange(half, dtype=jnp.float32) / half)
    ang = positions.astype(jnp.float32)[:, :, None, None] * inv_freq
    cos, sin = jnp.cos(ang), jnp.sin(ang)
    z1 = z[..., :half].astype(jnp.float32)
    z2 = z[..., half:].astype(jnp.float32)
    return jnp.concatenate([z1 * cos - z2 * sin, z1 * sin + z2 * cos], -1)


def _mlstm(q, k, v, o_pre, i_pre, f_pre, conv_w, gate_b, norm_g):
    B, T, _ = q.shape
    H, dh = N_MLSTM, HEAD_DIM
    qk = jax.nn.silu(_causal_conv(jnp.concatenate([q, k], -1), conv_w))
    q, k = jnp.split(qk, 2, axis=-1)
    qh, kh, vh = _heads(q, H), _heads(k, H) * dh ** -0.5, _heads(v, H)
    ig = (i_pre + gate_b[:H]).astype(jnp.float32).transpose(0, 2, 1)
    lf = jax.nn.log_sigmoid((f_pre + gate_b[H:]).astype(jnp.float32)).transpose(0, 2, 1)
    causal = jnp.tril(jnp.ones((CHUNK, CHUNK), bool))

    def step(carry, inp):
        C, n, m = carry
        qc, kc, vc, ic, fc = inp
        b = jnp.cumsum(fc, -1)
        dlog = jnp.where(causal, b[..., :, None] - b[..., None, :] + ic[..., None, :], -jnp.inf)
        inter = b + m[..., None]
        mt = jnp.maximum(dlog.max(-1), inter)
        s = jnp.einsum('bhtd,bhsd->bhts', qc, kc) * jnp.exp(dlog - mt[..., None])
        sc = jnp.exp(inter - mt)
        num = jnp.einsum('bhts,bhse->bhte', s, vc) + sc[..., None] * jnp.einsum('bhtd,bhde->bhte', qc, C)
        den = s.sum(-1) + sc * jnp.einsum('bhtd,bhd->bht', qc, n)
        h = num / jnp.maximum(jnp.abs(den), jnp.exp(-mt))[..., None]
        bl = b[..., -1]
        lw = bl[..., None] - b + ic
        m_new = jnp.maximum(bl + m, lw.max(-1))
        kw = kc * jnp.exp(lw - m_new[..., None])[..., None]
        dec = jnp.exp(bl + m - m_new)
        C = dec[..., None, None] * C + jnp.einsum('bhsd,bhse->bhde', kw, vc)
        n = dec[..., None] * n + kw.sum(-2)
        return (C, n, m_new), h

    init = (jnp.zeros((B, H, dh, dh), jnp.float32), jnp.zeros((B, H, dh), jnp.float32),
            jnp.zeros((B, H), jnp.float32))
    _, h = lax.scan(step, init, (_chunk(qh), _chunk(kh), _chunk(vh), _chunk(ig), _chunk(lf)))
    h = _head_norm(_unchunk(h), norm_g)
    return (h * jax.nn.sigmoid(o_pre.astype(jnp.float32))).astype(q.dtype)


def _rwkv7(z, mu, w0, w2, a0, a2, g2, k_k, k_a, r_k, ln_g, ln_b):
    B, T, _ = z.shape
    H, N = N_RWKV, RWKV_HEAD
    z = z + (_token_shift(z) - z) * mu
    r, k, v, wl, al, gl = _split(z, B_SIZES)
    w = -jax.nn.softplus(-(w0 + jnp.tanh(wl) @ w2).astype(jnp.float32)) - 0.5
    decay = jnp.exp(-jnp.exp(w))
    a = jax.nn.sigmoid((a0 + al @ a2).astype(jnp.float32))
    g = jax.nn.sigmoid(gl) @ g2
    kk = (k * k_k).astype(jnp.float32).reshape(B, T, H, N)
    kk = kk / jnp.maximum(jnp.linalg.norm(kk, axis=-1, keepdims=True), 1e-12)
    k = k.astype(jnp.float32) * (1.0 + (a - 1.0) * k_a)

    def hd(t):
        return t.astype(jnp.float32).reshape(B, T, H, N)

    rh, kh, vh, ah = hd(r), hd(k), hd(v), hd(a)

    def step(S, inp):
        rt, wt, kt, vt, at, bt = inp
        sa = jnp.einsum('bhij,bhj->bhi', S, at)
        S = S * wt[:, :, None, :] + sa[..., None] * bt[:, :, None, :] + vt[..., None] * kt[:, :, None, :]
        return S, jnp.einsum('bhij,bhj->bhi', S, rt)

    tm = lambda t: jnp.moveaxis(t, 1, 0)
    _, y = lax.scan(step, jnp.zeros((B, H, N, N), jnp.float32),
                    (tm(rh), tm(hd(decay)), tm(kh), tm(vh), tm(-kk), tm(kk * ah)))
    y = _head_norm(jnp.moveaxis(y, 0, 1), ln_g, ln_b, eps=RWKV_LN_EPS)
    bonus = ((rh * kh * r_k.reshape(H, N)).sum(-1, keepdims=True) * vh).reshape(B, T, -1)
    return ((y + bonus) * g).astype(z.dtype)


def _retention(q, k, v, g, positions, norm_g):
    B, T, _ = q.shape
    H, dh = N_RET, HEAD_DIM
    qh = _rotary(q.reshape(B, T, H, dh), positions).transpose(0, 2, 1, 3)
    kh = _rotary(k.reshape(B, T, H, dh), positions).transpose(0, 2, 1, 3) * dh ** -0.5
    vh = _heads(v, H)
    log_gamma = jnp.log1p(-jnp.exp2(-RET_GAMMA_BASE - jnp.arange(H, dtype=jnp.float32)))
    idx = jnp.arange(CHUNK, dtype=jnp.float32)
    causal = idx[:, None] >= idx[None, :]
    rel = jnp.where(causal, idx[:, None] - idx[None, :], 0.0)
    dmat = jnp.where(causal, jnp.exp(rel * log_gamma[:, None, None]), 0.0)
    zeta = jnp.exp((CHUNK - 1 - idx) * log_gamma[:, None])[:, :, None]
    xi = jnp.exp((idx + 1) * log_gamma[:, None])[:, :, None]
    chunk_decay = jnp.exp(CHUNK * log_gamma)[:, None, None]
    qc, kc, vc = _chunk(qh), _chunk(kh), _chunk(vh)
    intra = jnp.einsum('cbhts,cbhse->cbhte', jnp.einsum('cbhtd,cbhsd->cbhts', qc, kc) * dmat, vc)
    kv = jnp.einsum('cbhsd,cbhse->cbhde', kc * zeta, vc)

    def step(R, kv_c):
        return R * chunk_decay + kv_c, R

    _, r_prev = lax.scan(step, jnp.zeros((B, H, dh, dh), jnp.float32), kv)
    inter = jnp.einsum('cbhtd,cbhde->cbhte', qc, r_prev) * xi
    o = _head_norm(_unchunk(intra + inter), norm_g)
    return (o * jax.nn.silu(g.astype(jnp.float32))).astype(q.dtype)


def _gated_deltanet(q, k, v, z, a, b, conv_w, a_log, dt_bias, norm_g):
    B, T, _ = q.shape
    H, dh = N_GDN, HEAD_DIM
    f32 = jnp.float32
    qkv = jax.nn.silu(_causal_conv(jnp.concatenate([q, k, v], -1), conv_w))
    qs, ks, vs = jnp.split(qkv, 3, axis=-1)
    qh = _l2norm(_heads(qs, H)) * dh ** -0.5
    kh = _l2norm(_heads(ks, H))
    vh = _heads(vs, H)
    beta = jax.nn.sigmoid(b.astype(f32)).transpose(0, 2, 1)
    g = (-jnp.exp(a_log.astype(f32)) * jax.nn.softplus((a + dt_bias).astype(f32))).transpose(0, 2, 1)
    qc, kc, vc, bc = _chunk(qh), _chunk(kh), _chunk(vh), _chunk(beta)
    gc = jnp.cumsum(_chunk(g), -1)
    causal = jnp.tril(jnp.ones((CHUNK, CHUNK), bool))
    strict = jnp.tril(jnp.ones((CHUNK, CHUNK), bool), -1)
    gamma = jnp.exp(jnp.where(causal, gc[..., :, None] - gc[..., None, :], -jnp.inf))
    kb = kc * bc[..., None]
    ut = jnp.eye(CHUNK, dtype=f32) + jnp.where(strict, jnp.einsum('cbhtd,cbhsd->cbhts', kb, kc) * gamma, 0.0)
    u = lax.linalg.triangular_solve(ut, vc * bc[..., None], left_side=True, lower=True)
    w = lax.linalg.triangular_solve(ut, kb * jnp.exp(gc)[..., None], left_side=True, lower=True)
    att = jnp.einsum('cbhtd,cbhsd->cbhts', qc, kc) * gamma

    def step(S, inp):
        qi, ki, ui, wi, ai, gi = inp
        v_new = ui - jnp.einsum('bhtd,bhde->bhte', wi, S)
        o = (jnp.einsum('bhtd,bhde->bhte', qi * jnp.exp(gi)[..., None], S)
             + jnp.einsum('bhts,bhse->bhte', ai, v_new))
        g_last = gi[..., -1]
        S = (S * jnp.exp(g_last)[..., None, None]
             + jnp.einsum('bhsd,bhse->bhde', ki * jnp.exp(g_last[..., None] - gi)[..., None], v_new))
        return S, o

    _, o = lax.scan(step, jnp.zeros((B, H, dh, dh), f32), (qc, kc, u, w, att, gc))
    o = _head_norm(_unchunk(o), norm_g, eps=1e-6, center=False)
    return (o * jax.nn.silu(z.astype(f32))).astype(q.dtype)


def _mixer_ab(h, w_in, w_out, conv_w, gate_b, m_norm_g, mu, w0, w2, a0, a2, g2, k_k, k_a, r_k, r_ln_g, r_ln_b):
    proj = h @ w_in
    q, k, v, o, ig, fg = _split(proj[..., :A_COLS], A_SIZES)
    ya = _mlstm(q, k, v, o, ig, fg, conv_w, gate_b, m_norm_g)
    yb = _rwkv7(proj[..., A_COLS:], mu, w0, w2, a0, a2, g2, k_k, k_a, r_k, r_ln_g, r_ln_b)
    return jnp.concatenate([ya, yb], -1) @ w_out


def _mixer_cd(h, positions, w_in, w_out, ret_norm_g, conv_w, a_log, dt_bias, gdn_norm_g):
    proj = h @ w_in
    q, k, v, g = _split(proj[..., :C_COLS], C_SIZES)
    yc = _retention(q, k, v, g, positions, ret_norm_g)
    q, k, v, z, a, b = _split(proj[..., C_COLS:], D_SIZES)
    yd = _gated_deltanet(q, k, v, z, a, b, conv_w, a_log, dt_bias, gdn_norm_g)
    return jnp.concatenate([yc, yd], -1) @ w_out


def _ada(c, w, b):
    mod = jax.nn.silu(c) @ w + b
    shift, scale, gate = jnp.split(mod, 3, axis=-1)
    return shift[:, None], scale[:, None], gate[:, None]


def _swiglu(h, wg, wu, wd):
    return (jax.nn.silu(h @ wg) * (h @ wu)) @ wd


def setup_inputs(seed: int = 0) -> dict:
    key = jax.random.key(seed)
    ks = iter(jax.random.split(key, 48))
    f32 = jnp.float32
    NE, NO = (DEPTH + 1) // 2, DEPTH // 2

    def nrm(shape, scale):
        return jax.random.normal(next(ks), shape, f32) * scale

    def unif(shape, lo, hi):
        return jax.random.uniform(next(ks), shape, f32, lo, hi)

    x = nrm((BATCH, SEQ, D_MODEL), 1.0)
    c = nrm((BATCH, D_MODEL), 1.0)
    positions = (jax.random.randint(next(ks), (BATCH, 1), 0, 4096, jnp.int32)
                 + jnp.arange(SEQ, dtype=jnp.int32)[None, :])
    ada_w = nrm((DEPTH, 2, D_MODEL, 3 * D_MODEL), 0.1 * D_MODEL ** -0.5)
    ada_b = nrm((DEPTH, 2, 3 * D_MODEL), 0.02)
    ln_g = 1.0 + nrm((DEPTH, 2, D_MODEL), 0.02)
    ln_b = nrm((DEPTH, 2, D_MODEL), 0.02)
    ab_w_in = nrm((NE, D_MODEL, A_COLS + B_COLS), D_MODEL ** -0.5)
    ab_w_out = nrm((NE, 2 * D_GROUP, D_MODEL), BETA * (2 * D_GROUP) ** -0.5)
    mlstm_conv_w = nrm((NE, CONV_WIDTH, 2 * D_GROUP), CONV_WIDTH ** -0.5)
    mlstm_gate_b = jnp.concatenate([nrm((NE, N_MLSTM), 0.1),
                                    jnp.linspace(3.0, 6.0, N_MLSTM, dtype=f32) + nrm((NE, N_MLSTM), 0.1)], -1)
    mlstm_norm_g = 1.0 + nrm((NE, D_GROUP), 0.02)
    rwkv_mu = unif((NE, B_COLS), 0.0, 1.0)
    rwkv_w0 = unif((NE, D_GROUP), -6.0, -1.0)
    rwkv_w2 = nrm((NE, RWKV_DECAY_RANK, D_GROUP), 0.1 * RWKV_DECAY_RANK ** -0.5)
    rwkv_a0 = nrm((NE, D_GROUP), 0.1)
    rwkv_a2 = nrm((NE, RWKV_AAA_RANK, D_GROUP), RWKV_AAA_RANK ** -0.5)
    rwkv_g2 = nrm((NE, RWKV_GATE_RANK, D_GROUP), RWKV_GATE_RANK ** -0.5)
    rwkv_k_k = 0.85 + nrm((NE, D_GROUP), 0.05)
    rwkv_k_a = 1.0 + nrm((NE, D_GROUP), 0.05)
    rwkv_r_k = nrm((NE, D_GROUP), 0.1)
    rwkv_ln_g = 1.0 + nrm((NE, D_GROUP), 0.02)
    rwkv_ln_b = nrm((NE, D_GROUP), 0.02)
    cd_w_in = nrm((NO, D_MODEL, C_COLS + D_COLS), D_MODEL ** -0.5)
    cd_w_out = nrm((NO, 2 * D_GROUP, D_MODEL), BETA * (2 * D_GROUP) ** -0.5)
    ret_norm_g = 1.0 + nrm((NO, D_GROUP), 0.02)
    gdn_conv_w = nrm((NO, CONV_WIDTH, 3 * D_GROUP), CONV_WIDTH ** -0.5)
    gdn_a_log = jnp.log(unif((NO, N_GDN), 1.0, 16.0))
    dt = jnp.exp(unif((NO, N_GDN), -6.9, -2.3))
    gdn_dt_bias = dt + jnp.log(-jnp.expm1(-dt))
    gdn_norm_g = 1.0 + nrm((NO, D_GROUP), 0.02)
    ffn_w_gate = nrm((DEPTH, D_MODEL, D_FF), D_MODEL ** -0.5)
    ffn_w_up = nrm((DEPTH, D_MODEL, D_FF), D_MODEL ** -0.5)
    ffn_w_down = nrm((DEPTH, D_FF, D_MODEL), BETA * D_FF ** -0.5)
    return {'x': x, 'c': c, 'positions': positions,
            'ada_w': ada_w, 'ada_b': ada_b, 'ln_g': ln_g, 'ln_b': ln_b,
            'ab_w_in': ab_w_in, 'ab_w_out': ab_w_out, 'mlstm_conv_w': mlstm_conv_w,
            'mlstm_gate_b': mlstm_gate_b, 'mlstm_norm_g': mlstm_norm_g,
            'rwkv_mu': rwkv_mu, 'rwkv_w0': rwkv_w0, 'rwkv_w2': rwkv_w2, 'rwkv_a0': rwkv_a0,
            'rwkv_a2': rwkv_a2, 'rwkv_g2': rwkv_g2, 'rwkv_k_k': rwkv_k_k, 'rwkv_k_a': rwkv_k_a,
            'rwkv_r_k': rwkv_r_k, 'rwkv_ln_g': rwkv_ln_g, 'rwkv_ln_b': rwkv_ln_b,
            'cd_w_in': cd_w_in, 'cd_w_out': cd_w_out, 'ret_norm_g': ret_norm_g,
            'gdn_conv_w': gdn_conv_w, 'gdn_a_log': gdn_a_log, 'gdn_dt_bias': gdn_dt_bias,
            'gdn_norm_g': gdn_norm_g,
            'ffn_w_gate': ffn_w_gate, 'ffn_w_up': ffn_w_up, 'ffn_w_down': ffn_w_down}


def reference(x, c, positions, ada_w, ada_b, ln_g, ln_b,
              ab_w_in, ab_w_out, mlstm_conv_w, mlstm_gate_b, mlstm_norm_g,
              rwkv_mu, rwkv_w0, rwkv_w2, rwkv_a0, rwkv_a2, rwkv_g2, rwkv_k_k, rwkv_k_a,
              rwkv_r_k, rwkv_ln_g, rwkv_ln_b,
              cd_w_in, cd_w_out, ret_norm_g, gdn_conv_w, gdn_a_log, gdn_dt_bias, gdn_norm_g,
              ffn_w_gate, ffn_w_up, ffn_w_down):
    for layer in range(DEPTH):
        j = layer // 2
        shift, scale, gate = _ada(c, ada_w[layer, 0], ada_b[layer, 0])
        h = x * (1.0 + scale) + shift
        if layer % 2 == 0:
            y = _mixer_ab(h, ab_w_in[j], ab_w_out[j], mlstm_conv_w[j], mlstm_gate_b[j], mlstm_norm_g[j],
                          rwkv_mu[j], rwkv_w0[j], rwkv_w2[j], rwkv_a0[j], rwkv_a2[j], rwkv_g2[j],
                          rwkv_k_k[j], rwkv_k_a[j], rwkv_r_k[j], rwkv_ln_g[j], rwkv_ln_b[j])
        else:
            y = _mixer_cd(h, positions, cd_w_in[j], cd_w_out[j], ret_norm_g[j], gdn_conv_w[j],
                          gdn_a_log[j], gdn_dt_bias[j], gdn_norm_g[j])
        x = _layer_norm(ALPHA * x + (1.0 + gate) * y, ln_g[layer, 0], ln_b[layer, 0])
        shift, scale, gate = _ada(c, ada_w[layer, 1], ada_b[layer, 1])
        y = _swiglu(x * (1.0 + scale) + shift, ffn_w_gate[layer], ffn_w_up[layer], ffn_w_down[layer])
        x = _layer_norm(ALPHA * x + (1.0 + gate) * y, ln_g[layer, 1], ln_b[layer, 1])
    return x
```

```python
import os
import numpy as np
from contextlib import ExitStack
import concourse.bass as bass
import concourse.mybir as mybir
from concourse.bass_utils import run_bass_kernel_spmd

F32 = mybir.dt.float32
BF16 = mybir.dt.bfloat16
I32 = mybir.dt.int32
AF = mybir.ActivationFunctionType
ALU = mybir.AluOpType
AX = mybir.AxisListType

D = 2048
NB_ = 4
T = 2048
DFF = 5632
ALPHA = 4 ** 0.25
LN_EPS = 1e-5
NCORES = 8


class KB:
    def __init__(self, nc, es):
        self.nc = nc
        self.es = es
        self.engs = {"pe": nc.tensor, "dve": nc.vector, "act": nc.scalar, "pool": nc.gpsimd, "sp": nc.sync}
        self.esem = {}
        self.ecnt = {}
        for n in self.engs:
            self.esem[n] = es.enter_context(nc.semaphore("e_" + n))
            self.ecnt[n] = 0
        self.waited = {n: {} for n in self.engs}
        self.last_w = {}
        self.readers = {}
        self.dsem = {}
        self.nsem = 0

    def sb(self, name, shape, dt=F32):
        return self.es.enter_context(self.nc.sbuf_tensor(name, list(shape), dt))

    def ps(self, name, shape, dt=F32):
        return self.es.enter_context(self.nc.psum_tensor(name, list(shape), dt))

    @staticmethod
    def _k(key):
        if isinstance(key, tuple):
            return key[0], key[1]
        if isinstance(key, str):
            return key, None
        if hasattr(key, "tensor"):
            return key.tensor.name, None
        return key.name, None

    def _conf(self, table, key):
        name, sub = self._k(key)
        d = table.get(name)
        if not d:
            return []
        if sub is None:
            return list(d.values())
        out = []
        if sub in d:
            out.append(d[sub])
        if None in d:
            out.append(d[None])
        return out

    def _deps(self, reads, writes):
        toks = []
        for r in reads:
            toks += self._conf(self.last_w, r)
        for w in writes:
            toks += self._conf(self.last_w, w)
            for lst in self._conf(self.readers, w):
                toks += lst
        return toks

    def _emit_waits(self, eng, toks):
        need = {}
        for (sem, semname, val, src) in toks:
            if src == eng:
                if eng == "pe":
                    continue
                if val < self.ecnt[eng] - 2:
                    continue
            if self.waited[eng].get(semname, 0) >= val:
                continue
            if semname not in need or need[semname][1] < val:
                need[semname] = (sem, val)
        for semname, (sem, val) in need.items():
            self.engs[eng].wait_ge(sem, val)
            self.waited[eng][semname] = val

    def _record(self, tok, reads, writes):
        for w in writes:
            name, sub = self._k(w)
            if sub is None:
                self.last_w[name] = {None: tok}
                self.readers[name] = {}
            else:
                self.last_w.setdefault(name, {})[sub] = tok
                self.readers.setdefault(name, {})[sub] = []
        for r in reads:
            name, sub = self._k(r)
            self.readers.setdefault(name, {}).setdefault(sub, []).append(tok)

    def op(self, eng, fn, reads=(), writes=()):
        toks = self._deps(reads, writes)
        self._emit_waits(eng, toks)
        inst = fn()
        self.ecnt[eng] += 1
        inst.then_inc(self.esem[eng], 1)
        tok = (self.esem[eng], "e_" + eng, self.ecnt[eng], eng)
        self._record(tok, reads, writes)
        return inst

    def dma(self, q, out, in_, reads=None, writes=None, **kw):
        if reads is None:
            reads = [in_]
        if writes is None:
            writes = [out]
        toks = self._deps(reads, writes)
        self._emit_waits(q, toks)
        semkey = str(self._k(writes[0]))
        if semkey not in self.dsem:
            self.dsem[semkey] = [self.es.enter_context(self.nc.semaphore("d%d" % self.nsem)), 0]
            self.nsem += 1
        ent = self.dsem[semkey]
        inst = self.engs[q].dma_start(out=out, in_=in_, **kw)
        ent[1] += 16
        inst.then_inc(ent[0], 16)
        tok = (ent[0], "d_" + semkey, ent[1], "dma")
        self._record(tok, reads, writes)
        return inst

    def finish(self, eng="sp"):
        for semkey, (sem, cnt) in self.dsem.items():
            if cnt > 0:
                self.engs[eng].wait_ge(sem, cnt)
        for n in self.engs:
            if n != eng and self.ecnt[n] > 0:
                self.engs[eng].wait_ge(self.esem[n], self.ecnt[n])

    def _rw(self, out, ins, rk, wk):
        reads = rk if rk is not None else [a for a in ins if not isinstance(a, (int, float))]
        writes = wk if wk is not None else [out]
        return reads, writes

    def tt(self, out, a, b, op, eng="dve", rk=None, wk=None):
        r, w = self._rw(out, [a, b], rk, wk)
        e = self.engs[eng]
        return self.op(eng, lambda: e.tensor_tensor(out=out, in0=a, in1=b, op=op), r, w)

    def ts(self, out, a, s1, s2, op0, op1=None, eng="dve", rk=None, wk=None):
        r, w = self._rw(out, [a, s1, s2] if s2 is not None else [a, s1], rk, wk)
        e = self.engs[eng]
        if op1 is None:
            return self.op(eng, lambda: e.tensor_scalar(out=out, in0=a, scalar1=s1, scalar2=None, op0=op0), r, w)
        return self.op(eng, lambda: e.tensor_scalar(out=out, in0=a, scalar1=s1, scalar2=s2, op0=op0, op1=op1), r, w)

    def stt(self, out, a, s, b, op0, op1, rk=None, wk=None):
        r, w = self._rw(out, [a, s, b], rk, wk)
        return self.op("dve", lambda: self.nc.vector.scalar_tensor_tensor(out=out, in0=a, scalar=s, in1=b, op0=op0, op1=op1), r, w)

    def act(self, out, a, func, bias=None, scale=None, rk=None, wk=None):
        ins = [a] + [x for x in (bias, scale) if x is not None and not isinstance(x, (int, float))]
        r, w = self._rw(out, ins, rk, wk)
        kw = {}
        if bias is not None:
            kw["bias"] = bias
        if scale is not None:
            kw["scale"] = scale
        return self.op("act", lambda: self.nc.scalar.activation(out=out, in_=a, func=func, **kw), r, w)

    def copy(self, out, a, eng="dve", rk=None, wk=None):
        r, w = self._rw(out, [a], rk, wk)
        if eng == "act":
            return self.op("act", lambda: self.nc.scalar.copy(out=out, in_=a), r, w)
        e = self.engs[eng]
        return self.op(eng, lambda: e.tensor_copy(out=out, in_=a), r, w)

    def red(self, out, a, op=ALU.add, rk=None, wk=None):
        r, w = self._rw(out, [a], rk, wk)
        return self.op("dve", lambda: self.nc.vector.tensor_reduce(out=out, in_=a, axis=AX.X, op=op), r, w)

    def recip(self, out, a, rk=None, wk=None):
        r, w = self._rw(out, [a], rk, wk)
        return self.op("dve", lambda: self.nc.vector.reciprocal(out=out, in_=a), r, w)

    def memset(self, out, val, eng="dve", wk=None):
        e = self.engs[eng]
        return self.op(eng, lambda: e.memset(out, val), [], wk if wk is not None else [out])

    def mm(self, out, lhsT, rhs, start=True, stop=True, rk=None, wk=None):
        r, w = self._rw(out, [lhsT, rhs], rk, wk)
        return self.op("pe", lambda: self.nc.tensor.matmul(out, lhsT=lhsT, rhs=rhs, start=start, stop=stop), r, w)


def new_nc():
    return bass.Bass("TRN2", target_bir_lowering=False)


def din(nc, name, shape, dt=F32):
    return nc.dram_tensor(name, list(shape), dt, kind="ExternalInput").ap()


def dout(nc, name, shape, dt=F32):
    return nc.dram_tensor(name, list(shape), dt, kind="ExternalOutput").ap()


def run(nc, in_maps):
    res = run_bass_kernel_spmd(nc, in_maps, core_ids=list(range(len(in_maps))))
    return res.results


def tile_w(W):
    K, N = W.shape
    NBk = (N + 127) // 128
    if NBk * 128 != N:
        W = np.concatenate([W, np.zeros((K, NBk * 128 - N), W.dtype)], 1)
    KC = K // 128
    return np.ascontiguousarray(W.reshape(KC, 128, NBk, 128).transpose(2, 1, 0, 3))


def fm(X):
    Tn, C = X.shape
    return np.ascontiguousarray(X.T.reshape(C // 128, 128, Tn).transpose(1, 0, 2))


def unfm(Xt):
    P, CC, Tn = Xt.shape
    return np.ascontiguousarray(Xt.transpose(1, 0, 2).reshape(CC * 128, Tn).T)


def vec_fm(v):
    return np.ascontiguousarray(v.reshape(-1, 128).T)


def gemm(k, tag, actT, KC, wt, nblocks, NT, epilogue, ptiles, tg=512, wbufs=3):
    nc = k.nc
    wtiles = [k.sb("%s_w%d" % (tag, i), [128, KC, 128], BF16) for i in range(wbufs)]
    ngroups = (NT + tg - 1) // tg
    cnt = 0
    for nb in range(nblocks):
        slot = nb % wbufs
        wb = wtiles[slot]
        for c0 in range(0, KC, 16):
            c1 = min(KC, c0 + 16)
            k.dma("pool", wb[:, c0:c1, :], wt[nb, :, c0:c1, :], reads=[wt], writes=[(wb.name, c0)])
        for g in range(ngroups):
            t0, t1 = g * tg, min(NT, (g + 1) * tg)
            pt = ptiles[cnt % len(ptiles)]
            cnt += 1
            for kc in range(KC):
                k.mm(pt[:, 0:t1 - t0], wb[:, kc, :], actT[:, kc, t0:t1], start=(kc == 0), stop=(kc == KC - 1),
                     rk=[(wb.name, (kc // 16) * 16), actT], wk=[pt])
            epilogue(nb, g, t0, t1, pt)


def layer_norm_fm(k, xs, NT, gcol, bcol, ones, pS1, pS2, tmp):
    nc = k.nc
    KC = 16
    sq = tmp["sq"]
    for kc in range(KC):
        s = sq[kc % 2]
        k.act(s[:, 0:NT], xs[:, kc, :], AF.Square, rk=[(xs.name, kc)], wk=[s])
        k.mm(pS1[:, 0:NT], ones[:], xs[:, kc, :], start=(kc == 0), stop=(kc == KC - 1), rk=[ones, (xs.name, kc)], wk=[pS1])
        k.mm(pS2[:, 0:NT], ones[:], s[:, 0:NT], start=(kc == 0), stop=(kc == KC - 1), rk=[ones, s], wk=[pS2])
    mean, rstd, m2 = tmp["mean"], tmp["rstd"], tmp["m2"]
    k.ts(mean[:, 0:NT], pS1[:, 0:NT], 1.0 / D, None, ALU.mult)
    k.tt(m2[:, 0:NT], mean[:, 0:NT], mean[:, 0:NT], ALU.mult)
    k.stt(rstd[:, 0:NT], pS2[:, 0:NT], 1.0 / D, m2[:, 0:NT], ALU.mult, ALU.subtract)
    k.ts(rstd[:, 0:NT], rstd[:, 0:NT], LN_EPS, None, ALU.add)
    k.act(rstd[:, 0:NT], rstd[:, 0:NT], AF.Sqrt)
    k.recip(rstd[:, 0:NT], rstd[:, 0:NT])
    for kc in range(KC):
        xk = xs[:, kc, :]
        key = [(xs.name, kc)]
        k.tt(xk, xk, mean[:, 0:NT], ALU.subtract, rk=key + [mean], wk=key)
        k.tt(xk, xk, rstd[:, 0:NT], ALU.mult, rk=key + [rstd], wk=key)
        k.ts(xk, xk, gcol[:, kc:kc + 1], bcol[:, kc:kc + 1], ALU.mult, ALU.add, rk=key + [gcol, bcol], wk=key)


def build_A():
    nc = new_nc()
    cT = din(nc, "cT", [128, 16, NB_])
    wt = din(nc, "wt", [24, 128, 16, 128])
    bias = din(nc, "bias", [128, 24])
    o = dout(nc, "modT", [128, 24, NB_])
    with ExitStack() as es:
        k = KB(nc, es)
        c32 = k.sb("c32", [128, 16, NB_])
        cb = k.sb("cb", [128, 16, NB_], BF16)
        bt = k.sb("bt", [128, 24])
        ot = k.sb("ot", [128, 24, NB_])
        pts = [k.ps("ps%d" % i, [128, 512]) for i in range(4)]
        k.dma("sp", c32[:], cT)
        k.dma("sp", bt[:], bias)
        k.act(cb[:], c32[:], AF.Silu)

        def epi(nb, g, t0, t1, pt):
            k.act(ot[:, nb, :], pt[:, 0:NB_], AF.Identity, bias=bt[:, nb:nb + 1], rk=[pt, bt], wk=[(ot.name, nb)])

        gemm(k, "A", cb, 16, wt, 24, NB_, epi, pts)
        k.dma("sp", o, ot[:])
        k.finish()
    return nc


def run_A(inp):
    c = inp["c"]
    W = inp["ada_w"].reshape(4, D, 3 * D)
    bcat = inp["ada_b"].reshape(4 * 3 * D)
    Wt = np.concatenate([tile_w(W[s]) for s in range(4)], 0)
    cT = fm(c)
    nc = build_A()
    maps = []
    for i in range(NCORES):
        b = bcat[i * 24 * 128:(i + 1) * 24 * 128]
        maps.append({"cT": cT, "wt": np.ascontiguousarray(Wt[i * 24:(i + 1) * 24]), "bias": vec_fm(b)})
    res = run(nc, maps)
    modT = np.concatenate([r["modT"] for r in res], 1)
    mod = modT.transpose(2, 1, 0).reshape(NB_, 4, 3, D)
    return mod


def mod_fm(mod_bs):
    return np.ascontiguousarray(mod_bs.reshape(3, 16, 128).transpose(2, 0, 1))


def build_B(nblocks):
    nc = new_nc()
    NT = 1024
    xT = din(nc, "xT", [128, 16, NT])
    modv = din(nc, "modv", [128, 3, 16])
    wt = din(nc, "wt", [nblocks, 128, 16, 128])
    o = dout(nc, "projT", [nblocks, 128, NT])
    with ExitStack() as es:
        k = KB(nc, es)
        x32 = k.sb("x32", [128, 16, NT])
        hb = k.sb("hb", [128, 16, NT], BF16)
        mv = k.sb("mv", [128, 3, 16])
        stg = [k.sb("stg%d" % i, [128, NT]) for i in range(3)]
        pts = [k.ps("ps%d" % i, [128, 512]) for i in range(4)]
        k.dma("sp", mv[:], modv)
        for kc in range(16):
            k.dma("sp", x32[:, kc, :], xT[:, kc, :], writes=[(x32.name, kc)])
        k.ts(mv[:, 1, :], mv[:, 1, :], 1.0, None, ALU.add)
        for kc in range(16):
            k.ts(hb[:, kc, :], x32[:, kc, :], mv[:, 1, kc:kc + 1], mv[:, 0, kc:kc + 1], ALU.mult, ALU.add,
                 rk=[(x32.name, kc), mv], wk=[(hb.name, kc)])

        def epi(nb, g, t0, t1, pt):
            s = stg[nb % 3]
            k.copy(s[:, t0:t1], pt[:, 0:t1 - t0], eng="act", rk=[pt], wk=[(s.name, g)])
            if g == 1:
                k.dma("sp", o[nb], s[:], reads=[s], writes=[(o.tensor.name, nb % 3)])

        gemm(k, "B", hb, 16, wt, nblocks, NT, epi, pts)
        k.finish()
    return nc


def run_B(xT_cores, mod_sub, Wt):
    nblocks = Wt.shape[0]
    nc = build_B(nblocks)
    maps = []
    for i in range(NCORES):
        b = i // 2
        maps.append({"xT": xT_cores[i], "modv": mod_fm(mod_sub[b]), "wt": Wt})
    res = run(nc, maps)
    proj = np.empty((NB_, T, nblocks * 128), np.float32)
    for i in range(NCORES):
        b, s = i // 2, i % 2
        pT = res[i]["projT"]
        proj[b, s * 1024:(s + 1) * 1024] = pT.reshape(nblocks * 128, 1024).T
    return proj


def build_E():
    nc = new_nc()
    NT = 1024
    G = 512
    xT = din(nc, "xT", [128, 16, NT])
    yT = din(nc, "yT", [128, 16, NT])
    mod1 = din(nc, "mod1", [128, 3, 16])
    mod2 = din(nc, "mod2", [128, 3, 16])
    lnp = din(nc, "lnp", [128, 4, 16])
    wo = din(nc, "wo", [16, 128, 16, 128])
    wgu = din(nc, "wgu", [88, 128, 16, 128])
    wd = din(nc, "wd", [16, 128, 44, 128])
    o = dout(nc, "oT", [128, 16, NT])
    with ExitStack() as es:
        k = KB(nc, es)
        xs = k.sb("xs", [128, 16, G])
        y32 = k.sb("y32", [128, 4, G])
        ab = k.sb("ab", [128, 16, G], BF16)
        aT = k.sb("aT", [128, 44, G], BF16)
        m1 = k.sb("m1", [128, 3, 16]); m2 = k.sb("m2", [128, 3, 16]); lp = k.sb("lp", [128, 4, 16])
        ones = k.sb("ones", [128, 128])
        tmp = {"sq": [k.sb("sq0", [128, G]), k.sb("sq1", [128, G])], "mean": k.sb("mean", [128, G]),
               "rstd": k.sb("rstd", [128, G]), "m2": k.sb("m2t", [128, G])}
        et = [k.sb("et%d" % i, [128, G]) for i in range(2)]
        pts = [k.ps("ps%d" % i, [128, 512]) for i in range(4)]
        pS1 = k.ps("pS1", [128, 512]); pS2 = k.ps("pS2", [128, 512])
        k.memset(ones[:], 1.0)
        k.dma("sp", m1[:], mod1); k.dma("sp", m2[:], mod2); k.dma("sp", lp[:], lnp)
        k.ts(m1[:, 2, :], m1[:, 2, :], 1.0, None, ALU.add)
        k.ts(m2[:, 2, :], m2[:, 2, :], 1.0, None, ALU.add)
        k.ts(m2[:, 1, :], m2[:, 1, :], 1.0, None, ALU.add)
        ecnt = [0]

        def resid_epi(gate_col):
            def epi(nb, g, t0, t1, pt):
                e = et[ecnt[0] % 2]; ecnt[0] += 1
                k.ts(e[:], pt[:], gate_col[:, nb:nb + 1], None, ALU.mult, rk=[pt, gate_col], wk=[e])
                k.stt(xs[:, nb, :], xs[:, nb, :], ALPHA, e[:], ALU.mult, ALU.add, rk=[(xs.name, nb), e], wk=[(xs.name, nb)])
            return epi

        for gi in range(NT // G):
            t0 = gi * G
            for kc in range(16):
                k.dma("sp", xs[:, kc, :], xT[:, kc, t0:t0 + G], writes=[(xs.name, kc)])
            for q4 in range(4):
                k.dma("sp", y32[:], yT[:, q4 * 4:(q4 + 1) * 4, t0:t0 + G])
                k.copy(ab[:, q4 * 4:(q4 + 1) * 4, :], y32[:], eng="pool", wk=[(ab.name, q4)])
            gemm(k, "Eo%d" % gi, ab, 16, wo, 16, G, resid_epi(m1[:, 2, :]), pts)
            layer_norm_fm(k, xs, G, lp[:, 0, :], lp[:, 1, :], ones, pS1, pS2, tmp)
            for kc in range(16):
                k.ts(ab[:, kc, :], xs[:, kc, :], m2[:, 1, kc:kc + 1], m2[:, 0, kc:kc + 1], ALU.mult, ALU.add,
                     rk=[(xs.name, kc), m2], wk=[ab])
            hold = {}

            def gu_epi(nb, g, t0_, t1_, pt):
                if nb % 2 == 0:
                    e = et[ecnt[0] % 2]; ecnt[0] += 1
                    k.act(e[:], pt[:], AF.Silu)
                    hold["e"] = e
                else:
                    e = hold["e"]
                    k.tt(aT[:, nb // 2, :], e[:], pt[:], ALU.mult, rk=[e, pt], wk=[(aT.name, nb // 2)])

            gemm(k, "Eg%d" % gi, ab, 16, wgu, 88, G, gu_epi, pts)
            gemm(k, "Ed%d" % gi, aT, 44, wd, 16, G, resid_epi(m2[:, 2, :]), pts, wbufs=2)
            layer_norm_fm(k, xs, G, lp[:, 2, :], lp[:, 3, :], ones, pS1, pS2, tmp)
            for kc in range(16):
                k.dma("sp", o[:, kc, t0:t0 + G], xs[:, kc, :], reads=[(xs.name, kc)], writes=[(o.tensor.name, kc)])
        k.finish()
    return nc


def run_E(xT_cores, y_tok, mod1, mod2, ln_g, ln_b, w_out, wg, wu, wd):
    nc = build_E()
    wo_t = tile_w(w_out)
    g_t, u_t = tile_w(wg), tile_w(wu)
    wgu = np.empty((88,) + g_t.shape[1:], np.float32)
    wgu[0::2] = g_t
    wgu[1::2] = u_t
    wd_t = tile_w(wd)
    lnp = np.ascontiguousarray(np.stack([vec_fm(ln_g[0]), vec_fm(ln_b[0]), vec_fm(ln_g[1]), vec_fm(ln_b[1])], 1))
    maps = []
    for i in range(NCORES):
        b, s = i // 2, i % 2
        maps.append({"xT": xT_cores[i], "yT": fm(y_tok[b, s * 1024:(s + 1) * 1024]), "mod1": mod_fm(mod1[b]),
                     "mod2": mod_fm(mod2[b]), "lnp": lnp, "wo": wo_t, "wgu": wgu, "wd": wd_t})
    res = run(nc, maps)
    return [r["oT"] for r in res]


def build_K3(mixers, TB=8, T=T):
    nc = new_nc()
    with ExitStack() as es:
        k = KB(nc, es)
        ms = []
        for m in mixers:
            H, d, N = m["H"], m["d"], m["N"]
            n = m["name"]
            st = dict(m)
            st["Qf"] = din(nc, n + "_Qf", [d, T, H]); st["Wf"] = din(nc, n + "_Wf", [d, T, H])
            st["Kr"] = din(nc, n + "_Kr", [H, T, d]); st["Vb"] = din(nc, n + "_Vb", [H, T, N])
            if m["dplr"]:
                st["Af"] = din(nc, n + "_Af", [d, T, H]); st["Br"] = din(nc, n + "_Br", [H, T, d])
                st["mask"] = din(nc, n + "_mask", [H, N])
            st["Y"] = dout(nc, n + "_Y", [H, T, N])
            st["M"] = k.sb(n + "_M", [d, N])
            k.memset(st["M"][:], 0.0)
            nm = ["q", "w", "kr", "vb"] + (["a", "br"] if m["dplr"] else [])
            shp = {"q": [d, TB, H], "w": [d, TB, H], "a": [d, TB, H], "kr": [H, TB, d], "br": [H, TB, d], "vb": [H, TB, N]}
            st["buf"] = {x: [k.sb("%s_%s%d" % (n, x, i), shp[x]) for i in range(2)] for x in nm}
            st["ys"] = [k.sb("%s_ys%d" % (n, i), [H, TB, N]) for i in range(2)]
            nbank = (N + 511) // 512
            st["pdM"] = k.ps(n + "_pdM", [d, 512 * nbank])
            st["pY"] = k.ps(n + "_pY", [H, 512 * nbank])
            if m["dplr"]:
                st["pP"] = k.ps(n + "_pP", [H, 512 * nbank])
                st["Pb"] = k.sb(n + "_Pb", [H, N])
                st["mk"] = k.sb(n + "_mk", [H, N])
                k.dma("sp", st["mk"][:], st["mask"])
            st["segs"] = [(c0, min(N, c0 + 512)) for c0 in range(0, N, 512)]
            ms.append(st)
        src = {"q": "Qf", "w": "Wf", "a": "Af", "kr": "Kr", "br": "Br", "vb": "Vb"}
        for blk in range(T // TB):
            t0 = blk * TB
            par = blk % 2
            for st in ms:
                for x, tiles in st["buf"].items():
                    k.dma("sp", tiles[par][:], st[src[x]][:, t0:t0 + TB, :])
            for tt_ in range(TB):
                for st in ms:
                    H, d, N, dv, nx = st["H"], st["d"], st["N"], st["dv"], st["nx"]
                    M = st["M"]; bf = st["buf"]
                    if st["dplr"]:
                        for (c0, c1) in st["segs"]:
                            k.mm(st["pP"][:, c0:c1], bf["a"][par][:, tt_, :], M[:, c0:c1], rk=[bf["a"][par], M], wk=[st["pP"]])
                        k.tt(st["Pb"][:], st["pP"][:, 0:N], st["mk"][:], ALU.mult)
                        for (c0, c1) in st["segs"]:
                            k.mm(st["pdM"][:, c0:c1], bf["br"][par][:, tt_, :], st["Pb"][:, c0:c1], start=True, stop=False,
                                 rk=[bf["br"][par], st["Pb"]], wk=[st["pdM"]])
                            k.mm(st["pdM"][:, c0:c1], bf["kr"][par][:, tt_, :], bf["vb"][par][:, tt_, c0:c1], start=False, stop=True,
                                 rk=[bf["kr"][par], bf["vb"][par]], wk=[st["pdM"]])
                    else:
                        for (c0, c1) in st["segs"]:
                            k.mm(st["pdM"][:, c0:c1], bf["kr"][par][:, tt_, :], bf["vb"][par][:, tt_, c0:c1],
                                 rk=[bf["kr"][par], bf["vb"][par]], wk=[st["pdM"]])
                    w_t = bf["w"][par][:, tt_, :]
                    Mh = M[:, 0:H * dv].rearrange("p (h e) -> p h e", h=H)
                    k.tt(Mh, Mh, w_t.unsqueeze(2).broadcast_to([d, H, dv]), ALU.mult, rk=[M, bf["w"][par]], wk=[M])
                    if nx:
                        k.tt(M[:, H * dv:N], M[:, H * dv:N], w_t, ALU.mult, rk=[M, bf["w"][par]], wk=[M])
                    k.tt(M[:], M[:], st["pdM"][:, 0:N], ALU.add, rk=[M, st["pdM"]], wk=[M])
                    for (c0, c1) in st["segs"]:
                        k.mm(st["pY"][:, c0:c1], bf["q"][par][:, tt_, :], M[:, c0:c1], rk=[bf["q"][par], M], wk=[st["pY"]])
                    k.copy(st["ys"][par][:, tt_, :], st["pY"][:, 0:N], eng="act", rk=[st["pY"]], wk=[st["ys"][par]])
            for st in ms:
                k.dma("sp", st["Y"][:, t0:t0 + TB, :], st["ys"][par][:], reads=[st["ys"][par]],
                      writes=[(st["Y"].tensor.name, par)])
        k.finish()
    return nc


def blockmask(H, dv, nx):
    m = np.zeros((H, H * dv + nx), np.float32)
    for h in range(H):
        m[h, h * dv:(h + 1) * dv] = 1.0
        if nx:
            m[h, H * dv + h] = 1.0
    return m


def k3_pack(name, q, kk, v, w, a=None, b=None, ones_col=False):
    Tn, H, d = q.shape
    dv = v.shape[2]
    nx = H if ones_col else 0
    N = H * dv + nx
    out = {}
    out[name + "_Qf"] = np.ascontiguousarray(q.transpose(2, 0, 1))
    if w.ndim == 2:
        w = np.broadcast_to(w[:, :, None], (Tn, H, d))
    out[name + "_Wf"] = np.ascontiguousarray(w.transpose(2, 0, 1))
    out[name + "_Kr"] = np.ascontiguousarray(kk.transpose(1, 0, 2))
    Vb = np.zeros((H, Tn, N), np.float32)
    for h in range(H):
        Vb[h, :, h * dv:(h + 1) * dv] = v[:, h, :]
        if ones_col:
            Vb[h, :, H * dv + h] = 1.0
    out[name + "_Vb"] = Vb
    if a is not None:
        out[name + "_Af"] = np.ascontiguousarray(a.transpose(2, 0, 1))
        out[name + "_Br"] = np.ascontiguousarray(b.transpose(1, 0, 2))
        out[name + "_mask"] = blockmask(H, dv, nx)
    return out


def k3_unpack(Y, H, dv, ones_col=False):
    y = np.stack([Y[h, :, h * dv:(h + 1) * dv] for h in range(H)], 1)
    if ones_col:
        den = np.stack([Y[h, :, H * dv + h] for h in range(H)], 1)
        return y, den
    return y


NTK = 1024
NTI = NTK // 128


class Prog:
    def __init__(self):
        self.nc = new_nc()
        self.es = ExitStack()
        self.k = KB(self.nc, self.es)
        self.n = 0

    def inp(self, name, shape, dt=F32):
        return din(self.nc, name, shape, dt)

    def outp(self, name, shape, dt=F32):
        return dout(self.nc, name, shape, dt)

    def tile(self, shape, dt=F32, name=None):
        self.n += 1
        return self.k.sb(name or ("t%d" % self.n), shape, dt)

    def const(self, dram, shape):
        t = self.tile(shape)
        self.k.dma("sp", t[:], dram)
        return t

    def done(self):
        self.k.finish()
        self.es.close()
        return self.nc


def b3(ap2, P, H, dv):
    return ap2.unsqueeze(2).broadcast_to([P, H, dv])


def head_norm(k, x3, sq3, s, H, dv, eps, center):
    if center:
        k.red(s, x3)
        k.ts(s, s, -1.0 / dv, None, ALU.mult)
        k.tt(x3, x3, b3(s, 128, H, dv), ALU.add)
    k.tt(sq3, x3, x3, ALU.mult)
    k.red(s, sq3)
    k.ts(s, s, 1.0 / dv, eps, ALU.mult, ALU.add)
    k.act(s, s, AF.Sqrt)
    k.recip(s, s)
    k.tt(x3, x3, b3(s, 128, H, dv), ALU.mult)


def conv_silu(k, acc, tmp, zt, cwt):
    k.tt(acc[:], zt[0][:], cwt[:, 0, :], ALU.mult)
    for j in range(1, 4):
        k.tt(tmp[:], zt[j][:], cwt[:, j, :], ALU.mult)
        k.tt(acc[:], acc[:], tmp[:], ALU.add)
    k.act(acc[:], acc[:], AF.Silu)


def build_K2a():
    p = Prog(); k = p.k
    z = p.inp("z", [4, NTK, 2048]); cw = p.inp("cw", [128, 4, 2048]); gt = p.inp("gt", [NTK, 16])
    gb = p.inp("gb", [128, 16]); oin = p.inp("oin", [NTK, 1024])
    oq = p.outp("oq", [NTK, 1024]); ok = p.outp("ok", [NTK, 1024]); of = p.outp("of", [NTK, 8]); oog = p.outp("oog", [NTK, 1024])
    cwt = p.const(cw, [128, 4, 2048]); gbt = p.const(gb, [128, 16])
    zt = [p.tile([128, 2048]) for _ in range(4)]
    acc = p.tile([128, 2048]); tmp = p.tile([128, 2048]); g16 = p.tile([128, 16]); ki = p.tile([128, 8]); fo = p.tile([128, 8])
    ot = p.tile([128, 1024])
    for i in range(NTI):
        rows = slice(i * 128, (i + 1) * 128)
        for j in range(4):
            k.dma("sp", zt[j][:], z[j, rows, :])
        conv_silu(k, acc, tmp, zt, cwt)
        k.dma("sp", g16[:], gt[rows, :])
        k.tt(g16[:], g16[:], gbt[:], ALU.add)
        k.act(ki[:], g16[:, 0:8], AF.Exp, bias=float(np.log(128 ** -0.5)))
        k.act(fo[:], g16[:, 8:16], AF.Sigmoid)
        k3v = acc[:, 1024:2048].rearrange("p (h e) -> p h e", h=8)
        k.tt(k3v, k3v, b3(ki[:], 128, 8, 128), ALU.mult)
        k.dma("sp", oq[rows, :], acc[:, 0:1024]); k.dma("sp", ok[rows, :], acc[:, 1024:2048]); k.dma("sp", of[rows, :], fo[:])
        k.dma("sp", ot[:], oin[rows, :])
        k.act(ot[:], ot[:], AF.Sigmoid)
        k.dma("sp", oog[rows, :], ot[:])
    return p.done()


def build_K2b():
    p = Prog(); k = p.k; nc = p.nc
    rz = p.inp("rz", [2, NTK, 3072]); mub = p.inp("mub", [128, 3072]); zl = p.inp("zl", [2, 288, NTK]); mul = p.inp("mul", [288, 1])
    w2 = p.inp("w2", [64, 1024]); a2 = p.inp("a2", [64, 1024]); g2 = p.inp("g2", [160, 1024]); vb = p.inp("vb", [128, 5, 1024])
    names = ["o_r", "o_dec", "o_kf", "o_v", "o_A", "o_B", "o_g"]
    outs = {n: p.outp(n, [NTK, 1024]) for n in names}
    o_rk = p.outp("o_rk", [NTK, 16])
    mubt = p.const(mub, [128, 3072]); vbt = p.const(vb, [128, 5, 1024])
    w2t = p.const(w2, [64, 1024]); a2t = p.const(a2, [64, 1024]); g2a = p.const(g2[0:128, :], [128, 1024]); g2b = p.const(g2[128:160, :], [32, 1024])
    lr = []
    for (r0, r1, fn) in [(0, 64, AF.Tanh), (64, 128, None), (128, 256, AF.Sigmoid), (256, 288, AF.Sigmoid)]:
        n = r1 - r0
        c = p.const(zl[0, r0:r1, :], [n, NTK]); pv = p.const(zl[1, r0:r1, :], [n, NTK]); m = p.const(mul[r0:r1, :], [n, 1])
        k.tt(pv[:], pv[:], c[:], ALU.subtract)
        k.stt(c[:], pv[:], m[:, 0:1], c[:], ALU.mult, ALU.add)
        if fn is not None:
            k.act(c[:], c[:], fn)
        lr.append(c)
    wl, al, ga, gb_ = lr
    zc = p.tile([128, 3072]); zp = p.tile([128, 3072])
    dec = p.tile([128, 1024]); at = p.tile([128, 1024]); gtl = p.tile([128, 1024]); kk = p.tile([128, 1024]); sq = p.tile([128, 1024])
    At = p.tile([128, 1024]); Bt = p.tile([128, 1024]); t1 = p.tile([128, 1024]); kf = p.tile([128, 1024]); t2 = p.tile([128, 1024])
    ss = p.tile([128, 16]); rks = p.tile([128, 16])
    pw = k.ps("pw", [128, 1024]); pa = k.ps("pa", [128, 1024]); pg = k.ps("pg", [128, 1024])
    for i in range(NTI):
        rows = slice(i * 128, (i + 1) * 128)
        k.dma("sp", zc[:], rz[0, rows, :]); k.dma("sp", zp[:], rz[1, rows, :])
        k.tt(zp[:], zp[:], zc[:], ALU.subtract)
        k.tt(zp[:], zp[:], mubt[:], ALU.mult)
        k.tt(zc[:], zc[:], zp[:], ALU.add)
        r = zc[:, 0:1024]; kx = zc[:, 1024:2048]; v = zc[:, 2048:3072]
        for h2 in range(2):
            cs = slice(h2 * 512, (h2 + 1) * 512)
            k.mm(pw[:, cs], wl[:, rows], w2t[:, cs], rk=[wl, w2t], wk=[pw])
            k.mm(pa[:, cs], al[:, rows], a2t[:, cs], rk=[al, a2t], wk=[pa])
            k.mm(pg[:, cs], ga[:, rows], g2a[:, cs], start=True, stop=False, rk=[ga, g2a], wk=[pg])
            k.mm(pg[:, cs], gb_[:, rows], g2b[:, cs], start=False, stop=True, rk=[gb_, g2b], wk=[pg])
        k.tt(dec[:], pw[:], vbt[:, 0, :], ALU.add)
        k.act(dec[:], dec[:], AF.Sigmoid)
        k.act(dec[:], dec[:], AF.Exp, scale=-0.6065306597126334)
        k.tt(at[:], pa[:], vbt[:, 1, :], ALU.add)
        k.act(at[:], at[:], AF.Sigmoid)
        k.copy(gtl[:], pg[:], eng="act")
        k.tt(kk[:], kx, vbt[:, 2, :], ALU.mult)
        k.tt(sq[:], kk[:], kk[:], ALU.mult)
        k.red(ss[:], sq[:].rearrange("p (h e) -> p h e", h=16))
        k.act(ss[:], ss[:], AF.Sqrt)
        k.ts(ss[:], ss[:], 1e-12, None, ALU.max)
        k.recip(ss[:], ss[:])
        kk3 = kk[:].rearrange("p (h e) -> p h e", h=16)
        k.tt(kk3, kk3, b3(ss[:], 128, 16, 64), ALU.mult)
        k.ts(At[:], kk[:], -1.0, None, ALU.mult)
        k.tt(Bt[:], kk[:], at[:], ALU.mult)
        k.stt(t1[:], at[:], -1.0, vbt[:, 3, :], ALU.add, ALU.mult)
        k.ts(t1[:], t1[:], 1.0, None, ALU.add)
        k.tt(kf[:], kx, t1[:], ALU.mult)
        k.tt(t2[:], r, kf[:], ALU.mult)
        k.tt(t2[:], t2[:], vbt[:, 4, :], ALU.mult)
        k.red(rks[:], t2[:].rearrange("p (h e) -> p h e", h=16))
        for n, src in [("o_r", r), ("o_dec", dec[:]), ("o_kf", kf[:]), ("o_v", v), ("o_A", At[:]), ("o_B", Bt[:]), ("o_g", gtl[:])]:
            k.dma("sp", outs[n][rows, :], src)
        k.dma("sp", o_rk[rows, :], rks[:])
    return p.done()


def build_K4a():
    p = Prog(); k = p.k
    num = p.inp("num", [NTK, 1024]); den = p.inp("den", [NTK, 8]); og = p.inp("og", [NTK, 1024]); mg = p.inp("mg", [128, 1024])
    ry = p.inp("ry", [NTK, 1024]); rrk = p.inp("rrk", [NTK, 16]); rv = p.inp("rv", [NTK, 1024]); rg = p.inp("rg", [NTK, 1024])
    lng = p.inp("lng", [128, 1024]); lnb = p.inp("lnb", [128, 1024])
    y = p.outp("y", [NTK, 2048])
    mgt = p.const(mg, [128, 1024]); lngt = p.const(lng, [128, 1024]); lnbt = p.const(lnb, [128, 1024])
    x = p.tile([128, 1024]); sq = p.tile([128, 1024]); g = p.tile([128, 1024]); v = p.tile([128, 1024])
    d8 = p.tile([128, 8]); s8 = p.tile([128, 8]); s16 = p.tile([128, 16]); r16 = p.tile([128, 16])
    for i in range(NTI):
        rows = slice(i * 128, (i + 1) * 128)
        k.dma("sp", x[:], num[rows, :]); k.dma("sp", d8[:], den[rows, :]); k.dma("sp", g[:], og[rows, :])
        k.act(d8[:], d8[:], AF.Abs)
        k.ts(d8[:], d8[:], 1.0, None, ALU.max)
        k.recip(d8[:], d8[:])
        x3 = x[:].rearrange("p (h e) -> p h e", h=8)
        k.tt(x3, x3, b3(d8[:], 128, 8, 128), ALU.mult)
        head_norm(k, x3, sq[:].rearrange("p (h e) -> p h e", h=8), s8[:], 8, 128, 1e-5, True)
        k.tt(x[:], x[:], mgt[:], ALU.mult)
        k.tt(x[:], x[:], g[:], ALU.mult)
        k.dma("sp", y[rows, 0:1024], x[:])
        k.dma("sp", x[:], ry[rows, :]); k.dma("sp", r16[:], rrk[rows, :]); k.dma("sp", g[:], rg[rows, :]); k.dma("sp", v[:], rv[rows, :])
        x3 = x[:].rearrange("p (h e) -> p h e", h=16)
        head_norm(k, x3, sq[:].rearrange("p (h e) -> p h e", h=16), s16[:], 16, 64, 64e-5, True)
        k.tt(x[:], x[:], lngt[:], ALU.mult)
        k.tt(x[:], x[:], lnbt[:], ALU.add)
        v3 = v[:].rearrange("p (h e) -> p h e", h=16)
        k.tt(v3, v3, b3(r16[:], 128, 16, 64), ALU.mult)
        k.tt(x[:], x[:], v[:], ALU.add)
        k.tt(x[:], x[:], g[:], ALU.mult)
        k.dma("sp", y[rows, 1024:2048], x[:])
    return p.done()


TWO_PI = 2.0 * np.pi
CW1 = 6.28125
CW2 = TWO_PI - CW1


def sin_reduced(k, out, x, u, ni, m):
    k.ts(u, x, 1.0 / TWO_PI, None, ALU.mult)
    k.copy(ni, u)
    k.copy(u, ni)
    k.stt(out, u, -CW1, x, ALU.mult, ALU.add)
    k.stt(out, u, -CW2, out, ALU.mult, ALU.add)
    k.ts(m, out, float(np.pi), TWO_PI, ALU.is_gt, ALU.mult)
    k.tt(out, out, m, ALU.subtract)
    k.ts(m, out, float(-np.pi), TWO_PI, ALU.is_lt, ALU.mult)
    k.tt(out, out, m, ALU.add)
    k.ts(out, out, float(-np.pi), float(np.pi), ALU.max, ALU.min)
    k.act(out, out, AF.Sin)


def build_K2c():
    p = Prog(); k = p.k
    qi = p.inp("q", [NTK, 1024]); ki = p.inp("kx", [NTK, 1024]); gi = p.inp("gin", [NTK, 1024])
    pos = p.inp("pos", [NTK, 1], I32); invf = p.inp("invf", [128, 64])
    oq = p.outp("oq", [NTK, 1024]); ok = p.outp("ok", [NTK, 1024]); osg = p.outp("osg", [NTK, 1024])
    ift = p.const(invf, [128, 64])
    pi_ = p.tile([128, 1], I32); pf = p.tile([128, 1]); ang = p.tile([128, 64]); ang2 = p.tile([128, 64])
    S = p.tile([128, 64]); C = p.tile([128, 64]); Sk = p.tile([128, 64]); Ck = p.tile([128, 64])
    u = p.tile([128, 64]); m = p.tile([128, 64]); ni = p.tile([128, 64], I32)
    x = p.tile([128, 1024]); o = p.tile([128, 1024]); tmp = p.tile([128, 512]); g = p.tile([128, 1024])
    for i in range(NTI):
        rows = slice(i * 128, (i + 1) * 128)
        k.dma("sp", pi_[:], pos[rows, :])
        k.copy(pf[:], pi_[:])
        k.ts(ang[:], ift[:], pf[:, 0:1], None, ALU.mult)
        k.ts(ang2[:], ang[:], float(np.pi / 2), None, ALU.add)
        sin_reduced(k, S[:], ang[:], u[:], ni[:], m[:])
        sin_reduced(k, C[:], ang2[:], u[:], ni[:], m[:])
        k.ts(Sk[:], S[:], 128 ** -0.5, None, ALU.mult)
        k.ts(Ck[:], C[:], 128 ** -0.5, None, ALU.mult)
        for (src, dst, s_, c_) in [(qi, oq, S, C), (ki, ok, Sk, Ck)]:
            k.dma("sp", x[:], src[rows, :])
            x4 = x[:].rearrange("p (h two e) -> p h two e", h=8, two=2)
            o4 = o[:].rearrange("p (h two e) -> p h two e", h=8, two=2)
            t3_ = tmp[:].rearrange("p (h e) -> p h e", h=8)
            cb = c_[:].unsqueeze(1).broadcast_to([128, 8, 64]); sb_ = s_[:].unsqueeze(1).broadcast_to([128, 8, 64])
            x1, x2 = x4[:, :, 0, :], x4[:, :, 1, :]
            k.tt(o4[:, :, 0, :], x1, cb, ALU.mult, rk=[x, c_], wk=[o])
            k.tt(t3_, x2, sb_, ALU.mult, rk=[x, s_], wk=[tmp])
            k.tt(o4[:, :, 0, :], o4[:, :, 0, :], t3_, ALU.subtract, rk=[o, tmp], wk=[o])
            k.tt(o4[:, :, 1, :], x1, sb_, ALU.mult, rk=[x, s_], wk=[o])
            k.tt(t3_, x2, cb, ALU.mult, rk=[x, c_], wk=[tmp])
            k.tt(o4[:, :, 1, :], o4[:, :, 1, :], t3_, ALU.add, rk=[o, tmp], wk=[o])
            k.dma("sp", dst[rows, :], o[:])
        k.dma("sp", g[:], gi[rows, :])
        k.act(g[:], g[:], AF.Silu)
        k.dma("sp", osg[rows, :], g[:])
    return p.done()


def build_K2d():
    p = Prog(); k = p.k
    z = p.inp("z", [4, NTK, 3072]); cw = p.inp("cw", [128, 4, 3072]); ab = p.inp("ab", [NTK, 16])
    dtb = p.inp("dtb", [128, 8]); alog = p.inp("alog", [128, 8]); zin = p.inp("zin", [NTK, 1024])
    oq = p.outp("oq", [NTK, 1024]); okn = p.outp("okn", [NTK, 1024]); okp = p.outp("okp", [NTK, 1024]); oB = p.outp("oB", [NTK, 1024])
    ov = p.outp("ov", [NTK, 1024]); oal = p.outp("oal", [NTK, 8]); osz = p.outp("osz", [NTK, 1024])
    cwt = p.const(cw, [128, 4, 3072]); dtt = p.const(dtb, [128, 8]); nA = p.const(alog, [128, 8])
    k.act(nA[:], nA[:], AF.Exp)
    k.ts(nA[:], nA[:], -1.0, None, ALU.mult)
    zt = [p.tile([128, 3072]) for _ in range(4)]
    acc = p.tile([128, 3072]); tmp = p.tile([128, 3072])
    a16 = p.tile([128, 16]); beta = p.tile([128, 8]); al = p.tile([128, 8]); abn = p.tile([128, 8]); ss = p.tile([128, 8])
    kp = p.tile([128, 1024]); Bt = p.tile([128, 1024]); g = p.tile([128, 1024])
    for i in range(NTI):
        rows = slice(i * 128, (i + 1) * 128)
        for j in range(4):
            k.dma("sp", zt[j][:], z[j, rows, :])
        conv_silu(k, acc, tmp, zt, cwt)
        for (c0, scl) in [(0, 128 ** -0.5), (1024, 1.0)]:
            x3 = acc[:, c0:c0 + 1024].rearrange("p (h e) -> p h e", h=8)
            s3 = tmp[:, 0:1024].rearrange("p (h e) -> p h e", h=8)
            k.tt(s3, x3, x3, ALU.mult, rk=[acc], wk=[tmp])
            k.red(ss[:], s3, rk=[tmp], wk=[ss])
            k.ts(ss[:], ss[:], 1e-6, None, ALU.add)
            k.act(ss[:], ss[:], AF.Sqrt)
            k.recip(ss[:], ss[:])
            if scl != 1.0:
                k.ts(ss[:], ss[:], scl, None, ALU.mult)
            k.tt(x3, x3, b3(ss[:], 128, 8, 128), ALU.mult, rk=[acc, ss], wk=[acc])
        k.dma("sp", a16[:], ab[rows, :])
        k.act(beta[:], a16[:, 8:16], AF.Sigmoid)
        k.tt(al[:], a16[:, 0:8], dtt[:], ALU.add)
        k.act(al[:], al[:], AF.Exp)
        k.act(al[:], al[:], AF.Ln, bias=1.0)
        k.tt(al[:], al[:], nA[:], ALU.mult)
        k.act(al[:], al[:], AF.Exp)
        k.tt(abn[:], al[:], beta[:], ALU.mult)
        k.ts(abn[:], abn[:], -1.0, None, ALU.mult)
        kn3 = acc[:, 1024:2048].rearrange("p (h e) -> p h e", h=8)
        k.tt(kp[:].rearrange("p (h e) -> p h e", h=8), kn3, b3(beta[:], 128, 8, 128), ALU.mult, rk=[acc, beta], wk=[kp])
        k.tt(Bt[:].rearrange("p (h e) -> p h e", h=8), kn3, b3(abn[:], 128, 8, 128), ALU.mult, rk=[acc, abn], wk=[Bt])
        k.dma("sp", oq[rows, :], acc[:, 0:1024]); k.dma("sp", okn[rows, :], acc[:, 1024:2048]); k.dma("sp", ov[rows, :], acc[:, 2048:3072])
        k.dma("sp", okp[rows, :], kp[:]); k.dma("sp", oB[rows, :], Bt[:]); k.dma("sp", oal[rows, :], al[:])
        k.dma("sp", g[:], zin[rows, :])
        k.act(g[:], g[:], AF.Silu)
        k.dma("sp", osz[rows, :], g[:])
    return p.done()


def build_K4b():
    p = Prog(); k = p.k
    ro = p.inp("ro", [NTK, 1024]); rgn = p.inp("rgn", [128, 1024]); sg = p.inp("sg", [NTK, 1024])
    go = p.inp("go", [NTK, 1024]); ggn = p.inp("ggn", [128, 1024]); sz = p.inp("sz", [NTK, 1024])
    y = p.outp("y", [NTK, 2048])
    rgt = p.const(rgn, [128, 1024]); ggt = p.const(ggn, [128, 1024])
    x = p.tile([128, 1024]); sq = p.tile([128, 1024]); g = p.tile([128, 1024]); s8 = p.tile([128, 8])
    for i in range(NTI):
        rows = slice(i * 128, (i + 1) * 128)
        for (src, gsrc, gn, c0, eps, center) in [(ro, sg, rgt, 0, 1e-5, True), (go, sz, ggt, 1024, 1e-6, False)]:
            k.dma("sp", x[:], src[rows, :]); k.dma("sp", g[:], gsrc[rows, :])
            head_norm(k, x[:].rearrange("p (h e) -> p h e", h=8), sq[:].rearrange("p (h e) -> p h e", h=8), s8[:], 8, 128, eps, center)
            k.tt(x[:], x[:], gn[:], ALU.mult)
            k.tt(x[:], x[:], g[:], ALU.mult)
            k.dma("sp", y[rows, c0:c0 + 1024], x[:])
    return p.done()


def shift_tok(z, s):
    if s == 0:
        return z
    out = np.zeros_like(z)
    out[:, s:] = z[:, :-s]
    return out


def tok_cores(arr):
    return [np.ascontiguousarray(arr[i // 2, (i % 2) * NTK:(i % 2 + 1) * NTK]) for i in range(NCORES)]


def from_cores(res, name):
    out = np.empty((NB_, T) + res[0][name].shape[1:], np.float32)
    for i in range(NCORES):
        out[i // 2, (i % 2) * NTK:(i % 2 + 1) * NTK] = res[i][name]
    return out


def bc(v):
    return np.ascontiguousarray(np.broadcast_to(v[None], (128,) + v.shape)).astype(np.float32)


def run_tok(nc, per_core_inputs, shared):
    maps = []
    for i in range(NCORES):
        m = dict(shared)
        for kname, lst in per_core_inputs.items():
            m[kname] = lst[i]
        maps.append(m)
    return run(nc, maps)


L0_MIX = [dict(name="m", H=4, d=128, N=516, dv=128, nx=4, dplr=False),
          dict(name="r", H=8, d=64, N=512, dv=64, nx=0, dplr=True)]
L1_MIX = [dict(name="c", H=4, d=128, N=512, dv=128, nx=0, dplr=False),
          dict(name="g", H=4, d=128, N=512, dv=128, nx=0, dplr=True)]


def layer0_mixer(inp, proj):
    zqk = proj[..., 0:2048]
    zs = np.stack([shift_tok(zqk, 3 - j) for j in range(4)], 1)
    zc = [np.ascontiguousarray(zs[i // 2, :, (i % 2) * NTK:(i % 2 + 1) * NTK]) for i in range(NCORES)]
    resa = run_tok(build_K2a(), {"z": zc, "gt": tok_cores(proj[..., 4096:4112]), "oin": tok_cores(proj[..., 3072:4096])},
                   {"cw": bc(inp["mlstm_conv_w"][0]), "gb": bc(inp["mlstm_gate_b"][0])})
    mq, mk, mf, mog = (from_cores(resa, n) for n in ("oq", "ok", "of", "oog"))
    zB = proj[..., 4112:7472]
    zBp = shift_tok(zB, 1)
    rz = [np.ascontiguousarray(np.stack([zB[i // 2, (i % 2) * NTK:(i % 2 + 1) * NTK, 0:3072],
                                         zBp[i // 2, (i % 2) * NTK:(i % 2 + 1) * NTK, 0:3072]], 0)) for i in range(NCORES)]
    zl = [np.ascontiguousarray(np.stack([zB[i // 2, (i % 2) * NTK:(i % 2 + 1) * NTK, 3072:3360].T,
                                         zBp[i // 2, (i % 2) * NTK:(i % 2 + 1) * NTK, 3072:3360].T], 0)) for i in range(NCORES)]
    mu = inp["rwkv_mu"][0]
    vb = np.ascontiguousarray(np.stack([bc(inp[n][0]) for n in ("rwkv_w0", "rwkv_a0", "rwkv_k_k", "rwkv_k_a", "rwkv_r_k")], 1))
    resb = run_tok(build_K2b(), {"rz": rz, "zl": zl},
                   {"mub": bc(mu[0:3072]), "mul": np.ascontiguousarray(mu[3072:3360, None]), "w2": inp["rwkv_w2"][0],
                    "a2": inp["rwkv_a2"][0], "g2": inp["rwkv_g2"][0], "vb": vb})
    R = {n: from_cores(resb, n) for n in ("o_r", "o_dec", "o_kf", "o_v", "o_A", "o_B", "o_g", "o_rk")}
    mv = proj[..., 2048:3072]
    maps = []
    for i in range(NCORES):
        b, hh = i // 2, i % 2
        ms = slice(hh * 512, (hh + 1) * 512)
        m = k3_pack("m", mq[b][:, ms].reshape(T, 4, 128), mk[b][:, ms].reshape(T, 4, 128), mv[b][:, ms].reshape(T, 4, 128),
                    mf[b][:, hh * 4:(hh + 1) * 4], ones_col=True)
        m.update(k3_pack("r", R["o_r"][b][:, ms].reshape(T, 8, 64), R["o_kf"][b][:, ms].reshape(T, 8, 64),
                         R["o_v"][b][:, ms].reshape(T, 8, 64), R["o_dec"][b][:, ms].reshape(T, 8, 64),
                         a=R["o_A"][b][:, ms].reshape(T, 8, 64), b=R["o_B"][b][:, ms].reshape(T, 8, 64)))
        maps.append(m)
    res3 = run(build_K3(L0_MIX), maps)
    num = np.empty((NB_, T, 1024), np.float32); den = np.empty((NB_, T, 8), np.float32); ry = np.empty((NB_, T, 1024), np.float32)
    for i in range(NCORES):
        b, hh = i // 2, i % 2
        yv, dn = k3_unpack(res3[i]["m_Y"], 4, 128, ones_col=True)
        num[b, :, hh * 512:(hh + 1) * 512] = yv.reshape(T, 512)
        den[b, :, hh * 4:(hh + 1) * 4] = dn
        ry[b, :, hh * 512:(hh + 1) * 512] = k3_unpack(res3[i]["r_Y"], 8, 64).reshape(T, 512)
    res4 = run_tok(build_K4a(), {"num": tok_cores(num), "den": tok_cores(den), "og": tok_cores(mog), "ry": tok_cores(ry),
                                 "rrk": tok_cores(R["o_rk"]), "rv": tok_cores(R["o_v"]), "rg": tok_cores(R["o_g"])},
                   {"mg": bc(inp["mlstm_norm_g"][0]), "lng": bc(inp["rwkv_ln_g"][0]), "lnb": bc(inp["rwkv_ln_b"][0])})
    return from_cores(res4, "y")


def layer1_mixer(inp, proj):
    half = 64
    invf = (10000.0 ** (-np.arange(half, dtype=np.float32) / half)).astype(np.float32)
    pos = inp["positions"].astype(np.int32)[..., None]
    resc = run_tok(build_K2c(), {"q": tok_cores(proj[..., 0:1024]), "kx": tok_cores(proj[..., 1024:2048]),
                                 "gin": tok_cores(proj[..., 3072:4096]),
                                 "pos": [np.ascontiguousarray(pos[i // 2, (i % 2) * NTK:(i % 2 + 1) * NTK]) for i in range(NCORES)]},
                   {"invf": bc(invf)})
    cq, ck, csg = (from_cores(resc, n) for n in ("oq", "ok", "osg"))
    zq = proj[..., 4096:7168]
    zs = np.stack([shift_tok(zq, 3 - j) for j in range(4)], 1)
    zc = [np.ascontiguousarray(zs[i // 2, :, (i % 2) * NTK:(i % 2 + 1) * NTK]) for i in range(NCORES)]
    resd = run_tok(build_K2d(), {"z": zc, "ab": tok_cores(proj[..., 8192:8208]), "zin": tok_cores(proj[..., 7168:8192])},
                   {"cw": bc(inp["gdn_conv_w"][0]), "dtb": bc(inp["gdn_dt_bias"][0]), "alog": bc(inp["gdn_a_log"][0])})
    G = {n: from_cores(resd, n) for n in ("oq", "okn", "okp", "oB", "ov", "oal", "osz")}
    gam = (1.0 - np.exp2(-5.0 - np.arange(8, dtype=np.float32))).astype(np.float32)
    cv = proj[..., 2048:3072]
    maps = []
    for i in range(NCORES):
        b, hh = i // 2, i % 2
        ms = slice(hh * 512, (hh + 1) * 512)
        r4 = lambda a: a[b][:, ms].reshape(T, 4, 128)
        m = k3_pack("c", r4(cq), r4(ck), r4(cv), np.broadcast_to(gam[None, hh * 4:(hh + 1) * 4], (T, 4)))
        m.update(k3_pack("g", r4(G["oq"]), r4(G["okp"]), r4(G["ov"]), G["oal"][b][:, hh * 4:(hh + 1) * 4], a=r4(G["okn"]), b=r4(G["oB"])))
        maps.append(m)
    res3 = run(build_K3(L1_MIX), maps)
    ro = np.empty((NB_, T, 1024), np.float32); go = np.empty((NB_, T, 1024), np.float32)
    for i in range(NCORES):
        b, hh = i // 2, i % 2
        ro[b, :, hh * 512:(hh + 1) * 512] = k3_unpack(res3[i]["c_Y"], 4, 128).reshape(T, 512)
        go[b, :, hh * 512:(hh + 1) * 512] = k3_unpack(res3[i]["g_Y"], 4, 128).reshape(T, 512)
    res4 = run_tok(build_K4b(), {"ro": tok_cores(ro), "sg": tok_cores(csg), "go": tok_cores(go), "sz": tok_cores(G["osz"])},
                   {"rgn": bc(inp["ret_norm_g"][0]), "ggn": bc(inp["gdn_norm_g"][0])})
    return from_cores(res4, "y")


def kernel(**inputs):
    inp = {k_: np.asarray(v) for k_, v in inputs.items()}
    x = inp["x"].astype(np.float32)
    mod = run_A(inp)
    xT = [fm(x[i // 2, (i % 2) * NTK:(i % 2 + 1) * NTK]) for i in range(NCORES)]
    for layer in range(2):
        w_in = inp["ab_w_in"][0] if layer == 0 else inp["cd_w_in"][0]
        w_out = inp["ab_w_out"][0] if layer == 0 else inp["cd_w_out"][0]
        proj = run_B(xT, mod[:, 2 * layer], tile_w(w_in))
        y = layer0_mixer(inp, proj) if layer == 0 else layer1_mixer(inp, proj)
        if os.environ.get("KDBG"):
            np.save(os.path.join(os.environ["KDBG"], "proj%d.npy" % layer), proj)
            np.save(os.path.join(os.environ["KDBG"], "y%d.npy" % layer), y)
        xT = run_E(xT, y, mod[:, 2 * layer], mod[:, 2 * layer + 1], inp["ln_g"][layer], inp["ln_b"][layer], w_out,
                   inp["ffn_w_gate"][layer], inp["ffn_w_up"][layer], inp["ffn_w_down"][layer])
    out = np.empty((NB_, T, D), np.float32)
    for i in range(NCORES):
        out[i // 2, (i % 2) * NTK:(i % 2 + 1) * NTK] = unfm(xT[i])
    return out
```
